# Optimizing a Trainium2 kernel written in Bass

```python
import jax, jax.numpy as jnp
from jax import lax
import numpy as np

D_MODEL = 1024
BATCH = 4
SEQ = 4096
DEPTH = 2

D_FF = 2816
RET_HEADS = 4
RET_QK_DIM = 128
RET_V_DIM = 128
RET_CHUNK = 128
MLA_HEADS = 8
MLA_NOPE = 64
MLA_ROPE = 32
MLA_V = 64
Q_LORA = 256
KV_LORA = 128
Q_BLOCK = 128
ROPE_THETA = 10000.0
EPS = 1e-6
RET_Q_W = RET_HEADS * RET_QK_DIM
RET_V_W = RET_HEADS * RET_V_DIM
IN_SPLITS = [RET_Q_W, RET_Q_W, RET_V_W, RET_V_W, Q_LORA, KV_LORA, MLA_ROPE]
IN_WIDTH = sum(IN_SPLITS)
IN_OFFSETS = list(np.cumsum(IN_SPLITS)[:-1])
MIX_WIDTH = RET_HEADS * RET_V_DIM + MLA_HEADS * MLA_V

kernel_name = "hybrid_retention_mla_macaron"


def rms_norm(x, g):
    xf = x.astype(jnp.float32)
    y = xf * lax.rsqrt(jnp.mean(xf * xf, axis=-1, keepdims=True) + EPS)
    return (y * g.astype(jnp.float32)).astype(x.dtype)


def rope_tables(positions, dim):
    inv = 1.0 / (ROPE_THETA ** (jnp.arange(0, dim, 2, dtype=jnp.float32) / dim))
    ang = positions.astype(jnp.float32)[..., None] * inv
    return jnp.cos(ang)[:, :, None, :], jnp.sin(ang)[:, :, None, :]


def apply_rope(x, cos, sin):
    xf = x.astype(jnp.float32)
    x1, x2 = jnp.split(xf, 2, axis=-1)
    return jnp.concatenate([x1 * cos - x2 * sin, x2 * cos + x1 * sin], axis=-1).astype(x.dtype)


def swiglu_ffn(h, w_gate, w_up, w_down):
    return (jax.nn.silu(h @ w_gate) * (h @ w_up)) @ w_down


def retention(q, k, v, g, gn, cos, sin):
    out_dtype = v.dtype
    B, S, H, dk = q.shape
    n = S // RET_CHUNK
    C = RET_CHUNK
    q = apply_rope(q, cos, sin).astype(jnp.float32)
    k = apply_rope(k, cos, sin).astype(jnp.float32) * (dk ** -0.5)
    v = v.astype(jnp.float32)

    def chunk(t):
        return t.reshape(B, n, C, H, -1).transpose(0, 3, 1, 2, 4)

    qc, kc, vc = chunk(q), chunk(k), chunk(v)
    lg = jnp.log(1.0 - 2.0 ** (-5.0 - jnp.arange(H, dtype=jnp.float32)))
    idx = jnp.arange(C, dtype=jnp.float32)
    rel = idx[:, None] - idx[None, :]
    decay = jnp.where(rel >= 0, jnp.exp(jnp.maximum(rel, 0.0)[None] * lg[:, None, None]), 0.0)
    scores = jnp.einsum('bhncd,bhnmd->bhncm', qc, kc) * decay[None, :, None]
    y_intra = jnp.einsum('bhncm,bhnme->bhnce', scores, vc)
    zeta = jnp.exp((C - 1 - idx)[None, :] * lg[:, None])
    chunk_kv = jnp.einsum('bhncd,bhnce->bhnde', kc * zeta[None, :, None, :, None], vc)
    chunk_decay = jnp.exp(C * lg)[None, :, None, None]

    def step(state, kv_n):
        return state * chunk_decay + kv_n, state

    init = jnp.zeros((B, H, dk, vc.shape[-1]), jnp.float32)
    _, prev = lax.scan(step, init, jnp.moveaxis(chunk_kv, 2, 0))
    prev = jnp.moveaxis(prev, 0, 2)
    xi = jnp.exp((idx + 1.0)[None, :] * lg[:, None])
    y_cross = jnp.einsum('bhncd,bhnde->bhnce', qc, prev) * xi[None, :, None, :, None]
    y = (y_intra + y_cross).transpose(0, 2, 3, 1, 4).reshape(B, S, H, -1)
    y = rms_norm(y, gn) * jax.nn.silu(g.astype(jnp.float32))
    return y.reshape(B, S, -1).astype(out_dtype)


def mla_attention(c_q, c_kv, k_rope, q_lat_norm, w_uq, kv_lat_norm, w_ukv,
                  qn_nope, qn_rope, kn_nope, kn_rope, cos, sin):
    B, S, _ = c_q.shape
    q = (rms_norm(c_q, q_lat_norm) @ w_uq).reshape(B, S, MLA_HEADS, MLA_NOPE + MLA_ROPE)
    kv = (rms_norm(c_kv, kv_lat_norm) @ w_ukv).reshape(B, S, MLA_HEADS, MLA_NOPE + MLA_V)
    q_nope, q_rope = q[..., :MLA_NOPE], q[..., MLA_NOPE:]
    k_nope, v = kv[..., :MLA_NOPE], kv[..., MLA_NOPE:]
    q_nope = rms_norm(q_nope, qn_nope)
    q_rope = apply_rope(rms_norm(q_rope, qn_rope), cos, sin)
    k_nope = rms_norm(k_nope, kn_nope)
    k_rope = apply_rope(rms_norm(k_rope[:, :, None, :], kn_rope), cos, sin)
    k_rope = jnp.broadcast_to(k_rope, (B, S, MLA_HEADS, MLA_ROPE))
    qf = jnp.concatenate([q_nope, q_rope], -1).transpose(0, 2, 1, 3).astype(jnp.float32)
    kf = jnp.concatenate([k_nope, k_rope], -1).transpose(0, 2, 1, 3).astype(jnp.float32)
    vf = v.transpose(0, 2, 1, 3).astype(jnp.float32)
    scale = (MLA_NOPE + MLA_ROPE) ** -0.5
    n_blk = S // Q_BLOCK
    q_blocks = qf.reshape(B, MLA_HEADS, n_blk, Q_BLOCK, -1).transpose(2, 0, 1, 3, 4)
    key_pos = jnp.arange(S)

    def block(args):
        qb, bi = args
        s = jnp.einsum('bhqd,bhkd->bhqk', qb, kf) * scale
        qpos = bi * Q_BLOCK + jnp.arange(Q_BLOCK)
        s = jnp.where(qpos[:, None] >= key_pos[None, :], s, -1e30)
        p = jax.nn.softmax(s, axis=-1)
        return jnp.einsum('bhqk,bhkd->bhqd', p, vf)

    o = lax.map(block, (q_blocks, jnp.arange(n_blk)))
    o = o.transpose(1, 0, 3, 2, 4).reshape(B, S, MLA_HEADS * MLA_V)
    return o.astype(c_q.dtype)


def setup_inputs(seed: int = 0) -> dict:
    key = jax.random.key(seed)
    ks = iter(jax.random.split(key, 32))
    L, D, F = DEPTH, D_MODEL, D_FF

    def w(shape, fan_in):
        return jax.random.normal(next(ks), shape, jnp.float32) * (fan_in ** -0.5)

    def gain(shape):
        return 1.0 + 0.02 * jax.random.normal(next(ks), shape, jnp.float32)

    x = jax.random.normal(next(ks), (BATCH, SEQ, D), jnp.float32)
    positions = jnp.broadcast_to(jnp.arange(SEQ, dtype=jnp.int32)[None], (BATCH, SEQ))
    return {
        "x": x,
        "positions": positions,
        "ffn1_norm": gain((L, D)),
        "ffn1_w_gate": w((L, D, F), D),
        "ffn1_w_up": w((L, D, F), D),
        "ffn1_w_down": w((L, F, D), F),
        "mix_norm": gain((L, D)),
        "w_in": w((L, D, IN_WIDTH), D),
        "ret_head_norm": gain((L, RET_HEADS, RET_V_DIM)),
        "q_lat_norm": gain((L, Q_LORA)),
        "w_uq": w((L, Q_LORA, MLA_HEADS * (MLA_NOPE + MLA_ROPE)), Q_LORA),
        "kv_lat_norm": gain((L, KV_LORA)),
        "w_ukv": w((L, KV_LORA, MLA_HEADS * (MLA_NOPE + MLA_V)), KV_LORA),
        "qn_nope": gain((L, MLA_NOPE)),
        "qn_rope": gain((L, MLA_ROPE)),
        "kn_nope": gain((L, MLA_NOPE)),
        "kn_rope": gain((L, MLA_ROPE)),
        "w_o": w((L, MIX_WIDTH, D), MIX_WIDTH),
        "ffn2_norm": gain((L, D)),
        "ffn2_w_gate": w((L, D, F), D),
        "ffn2_w_up": w((L, D, F), D),
        "ffn2_w_down": w((L, F, D), F),
    }


def reference(x, positions, ffn1_norm, ffn1_w_gate, ffn1_w_up, ffn1_w_down, mix_norm,
              w_in, ret_head_norm, q_lat_norm, w_uq, kv_lat_norm, w_ukv,
              qn_nope, qn_rope, kn_nope, kn_rope, w_o,
              ffn2_norm, ffn2_w_gate, ffn2_w_up, ffn2_w_down):
    B, S, _ = x.shape
    cos_r, sin_r = rope_tables(positions, RET_QK_DIM)
    cos_m, sin_m = rope_tables(positions, MLA_ROPE)
    for l in range(DEPTH):
        x = x + 0.5 * swiglu_ffn(rms_norm(x, ffn1_norm[l]), ffn1_w_gate[l], ffn1_w_up[l], ffn1_w_down[l])
        h = rms_norm(x, mix_norm[l])
        proj = h @ w_in[l]
        rq, rk, rv, rg, c_q, c_kv, k_rope = jnp.split(proj, IN_OFFSETS, axis=-1)
        y_ret = retention(rq.reshape(B, S, RET_HEADS, RET_QK_DIM), rk.reshape(B, S, RET_HEADS, RET_QK_DIM),
                          rv.reshape(B, S, RET_HEADS, RET_V_DIM), rg.reshape(B, S, RET_HEADS, RET_V_DIM),
                          ret_head_norm[l], cos_r, sin_r)
        y_mla = mla_attention(c_q, c_kv, k_rope, q_lat_norm[l], w_uq[l], kv_lat_norm[l], w_ukv[l],
                              qn_nope[l], qn_rope[l], kn_nope[l], kn_rope[l], cos_m, sin_m)
        x = x + jnp.concatenate([y_ret, y_mla], axis=-1) @ w_o[l]
        x = x + 0.5 * swiglu_ffn(rms_norm(x, ffn2_norm[l]), ffn2_w_gate[l], ffn2_w_up[l], ffn2_w_down[l])
    return x
```

```python
import numpy as np
import concourse.bass as bass
import concourse.mybir as mybir

F32 = mybir.dt.float32
BF16 = mybir.dt.bfloat16
I32 = mybir.dt.int32
U8 = mybir.dt.uint8
AF = mybir.ActivationFunctionType
ALU = mybir.AluOpType
AX = mybir.AxisListType

ENGS = ["pe", "act", "dve", "pool", "sp"]
NDSEM = 8


class Sched:
    def __init__(self, nc, same_engine_sync=True):
        self.nc = nc
        self.ops = {e: [] for e in ENGS}
        self.last_w = {}
        self.readers = {}
        self.same_engine_sync = same_engine_sync
        self.ndma = {e: 0 for e in ENGS}

    def op(self, eng, fn, reads=(), writes=(), dma=False):
        idx = len(self.ops[eng])
        ref = (eng, idx)
        deps = set()
        for k in reads:
            w = self.last_w.get(k)
            if w is not None:
                deps.add(w)
        for k in writes:
            w = self.last_w.get(k)
            if w is not None:
                deps.add(w)
            for r in self.readers.get(k, ()):
                deps.add(r)
        deps.discard(ref)
        rec = dict(fn=fn, deps=deps, dma=dma, inc=False, dj=None)
        if dma:
            rec["dj"] = self.ndma[eng]
            self.ndma[eng] += 1
        self.ops[eng].append(rec)
        for k in writes:
            self.last_w[k] = ref
            self.readers[k] = []
        for k in reads:
            self.readers.setdefault(k, []).append(ref)
        return ref

    def emit(self, final_wait_eng="sp"):
        nc = self.nc
        ops = self.ops
        for e in ENGS:
            for i, rec in enumerate(ops[e]):
                for (e2, i2) in rec["deps"]:
                    d = ops[e2][i2]
                    if d["dma"]:
                        continue
                    if e2 == e and (e == "pe" or not self.same_engine_sync):
                        continue
                    d["inc"] = True
        for e in ENGS:
            c = 0
            for rec in ops[e]:
                if not rec["dma"] and rec["inc"]:
                    c += 1
                rec["cnt"] = c
        import contextlib
        with contextlib.ExitStack() as st:
            EP = 200
            nep = {e: max(1, (max([r["cnt"] for r in ops[e]] + [0]) + EP - 1) // EP) for e in ENGS}
            esem = {e: [st.enter_context(nc.semaphore("es_%s_%d" % (e, k))) for k in range(nep[e])] for e in ENGS}
            dsem = {e: [st.enter_context(nc.semaphore("ds_%s_%d" % (e, j))) for j in range(NDSEM)]
                    for e in ENGS if self.ndma[e] > 0}
            block = st.enter_context(nc.Block())

            def dma_sem_target(e2, rec2):
                j = rec2["dj"]
                return dsem[e2][j % NDSEM], 16 * (j // NDSEM + 1)

            def run(e, eng):
                waited = {}

                def wait(sem, key, val):
                    if waited.get(key, 0) >= val:
                        return
                    eng.wait_ge(sem, val)
                    waited[key] = val

                for i, rec in enumerate(ops[e]):
                    for (e2, i2) in sorted(rec["deps"]):
                        d = ops[e2][i2]
                        if d["dma"]:
                            s, t = dma_sem_target(e2, d)
                            wait(s, ("d", e2, d["dj"] % NDSEM), t)
                        else:
                            if e2 == e and (e == "pe" or not self.same_engine_sync):
                                continue
                            k = d["cnt"] - 1
                            wait(esem[e2][k // EP], ("e", e2, k // EP), k % EP + 1)
                    if rec["dma"]:
                        j = rec["dj"]
                        s, t = dma_sem_target(e, rec)
                        if j >= NDSEM:
                            wait(s, ("d", e, j % NDSEM), t - 16)
                        rec["fn"](eng).then_inc(s, 16)
                    else:
                        ins = rec["fn"](eng)
                        if rec["inc"]:
                            ins.then_inc(esem[e][(rec["cnt"] - 1) // EP], 1)
                if e == final_wait_eng:
                    for e2 in ENGS:
                        n = self.ndma[e2]
                        for j in range(max(0, n - NDSEM), n):
                            wait(dsem[e2][j % NDSEM], ("d", e2, j % NDSEM), 16 * (j // NDSEM + 1))

            @block.tensor
            def _(eng):
                run("pe", eng)

            @block.scalar
            def _(eng):
                run("act", eng)

            @block.vector
            def _(eng):
                run("dve", eng)

            @block.gpsimd
            def _(eng):
                run("pool", eng)

            @block.sync
            def _(eng):
                run("sp", eng)


D = 1024
F = 2816
NF = F // 128
NDC = D // 128
NT = 2048
TT = 512
NTT = NT // TT
EPS = 1e-6


class Ctx:
    pass


def alloc_common(nc, S):
    C = Ctx()
    C.nc = nc
    C.S = S
    C.ps = [nc.alloc_psum_tensor("ps%d" % b, [128, 512], F32) for b in range(8)]
    C.xT = nc.alloc_sbuf_tensor("xT", [128, NDC, NT], F32)
    C.hn = nc.alloc_sbuf_tensor("hnT", [128, NDC, NT], BF16)
    C.act = nc.alloc_sbuf_tensor("actG", [128, NF // 2, NT], BF16)
    C.wgu = [nc.alloc_sbuf_tensor("wgu_sb%d" % i, [128, 2, NDC, 128], BF16) for i in range(3)]
    C.wd = [nc.alloc_sbuf_tensor("wd_sb%d" % i, [128, 256], BF16) for i in range(4)]
    C.sq = [nc.alloc_sbuf_tensor("sq%d" % i, [128, TT], BF16) for i in range(2)]
    C.rstd = [nc.alloc_sbuf_tensor("rstd%d" % i, [128, TT], F32) for i in range(2)]
    C.sil = [nc.alloc_sbuf_tensor("sil%d" % i, [128, TT], F32) for i in range(2)]
    C.ones = nc.alloc_sbuf_tensor("ones", [128, 128], BF16)
    C.gains = nc.alloc_sbuf_tensor("gains_sb", [128, 3, NDC], F32)
    C.cnt = dict(wgu=0, wd=0, tmp=0, ps=0)
    S.op("pool", lambda e: e.memset(C.ones[:], 1.0), writes=[("ones",)])
    return C


def rmsnorm_T(C, gain_idx, out_tile, out_key):
    S = C.S
    for t in range(NTT):
        tsl = slice(t * TT, (t + 1) * TT)
        b = C.cnt["ps"] % 2
        C.cnt["ps"] += 1
        psb = C.ps[b]
        for dc in range(NDC):
            i = C.cnt["tmp"] % 2
            C.cnt["tmp"] += 1
            sq = C.sq[i]
            S.op("act", lambda e, sq=sq, dc=dc, tsl=tsl: e.activation(sq[:], C.xT[:, dc, tsl], AF.Square),
                 reads=[("x", dc, t)], writes=[("sq", i)])
            S.op("pe", lambda e, sq=sq, dc=dc, psb=psb: e.matmul(psb[:], C.ones[:], sq[:], start=(dc == 0), stop=(dc == NDC - 1)),
                 reads=[("sq", i), ("ones",)], writes=[("ps", b)])
        r = C.cnt["tmp"] % 2
        rs = C.rstd[r]
        S.op("act", lambda e, rs=rs, psb=psb: e.activation(rs[:], psb[:], AF.Sqrt, bias=C.epsb[:], scale=1.0 / D),
             reads=[("ps", b), ("epsb",)], writes=[("rstd", r)])
        S.op("dve", lambda e, rs=rs: e.reciprocal(rs[:], rs[:]), reads=[("rstd", r)], writes=[("rstd", r)])
        for dc in range(NDC):
            S.op("dve", lambda e, rs=rs, dc=dc, tsl=tsl: e.scalar_tensor_tensor(
                out_tile[:, dc, tsl], C.xT[:, dc, tsl], C.gains[:, gain_idx, dc:dc + 1], rs[:], ALU.mult, ALU.mult),
                reads=[("x", dc, t), ("rstd", r), ("gains",)], writes=[(out_key, dc, t)])


def ffn(C, wgu_d, wd_d, gain_idx):
    S = C.S
    rmsnorm_T(C, gain_idx, C.hn, "hn")
    HG = NF // 2
    for G in range(2):
        for fi in range(HG):
            f = G * HG + fi
            wb = C.cnt["wgu"] % 3
            C.cnt["wgu"] += 1
            wt = C.wgu[wb]
            S.op("pool", lambda e, wt=wt, f=f: e.dma_start(out=wt[:], in_=wgu_d[f]),
                 writes=[("wgu", wb)], dma=True)
            for t in range(NTT):
                tsl = slice(t * TT, (t + 1) * TT)
                pb = (C.cnt["ps"] % 2) * 2
                C.cnt["ps"] += 1
                pg, pu = C.ps[pb], C.ps[pb + 1]
                for dc in range(NDC):
                    S.op("pe", lambda e, wt=wt, dc=dc, tsl=tsl, pg=pg: e.matmul(
                        pg[:], wt[:, 0, dc, :], C.hn[:, dc, tsl], start=(dc == 0), stop=(dc == NDC - 1)),
                        reads=[("wgu", wb), ("hn", dc, t)], writes=[("ps", pb)])
                for dc in range(NDC):
                    S.op("pe", lambda e, wt=wt, dc=dc, tsl=tsl, pu=pu: e.matmul(
                        pu[:], wt[:, 1, dc, :], C.hn[:, dc, tsl], start=(dc == 0), stop=(dc == NDC - 1)),
                        reads=[("wgu", wb), ("hn", dc, t)], writes=[("ps", pb + 1)])
                si = C.cnt["tmp"] % 2
                C.cnt["tmp"] += 1
                sl = C.sil[si]
                S.op("act", lambda e, sl=sl, pg=pg: e.activation(sl[:], pg[:], AF.Silu),
                     reads=[("ps", pb)], writes=[("sil", si)])
                S.op("dve", lambda e, sl=sl, pu=pu, fi=fi, tsl=tsl: e.tensor_tensor(
                    C.act[:, fi, tsl], sl[:], pu[:], ALU.mult),
                    reads=[("sil", si), ("ps", pb + 1)], writes=[("act", fi, t)])
        for dg in range(4):
            for fi in range(HG):
                f = G * HG + fi
                db = C.cnt["wd"] % 4
                C.cnt["wd"] += 1
                wdt = C.wd[db]
                S.op("pool", lambda e, wdt=wdt, dg=dg, f=f: e.dma_start(out=wdt[:], in_=wd_d[dg, f]),
                     writes=[("wd", db)], dma=True)
                for j in range(2):
                    for t in range(NTT):
                        tsl = slice(t * TT, (t + 1) * TT)
                        b = j * 4 + t
                        S.op("pe", lambda e, wdt=wdt, j=j, fi=fi, tsl=tsl, b=b: e.matmul(
                            C.ps[b][:], wdt[:, j * 128:(j + 1) * 128], C.act[:, fi, tsl],
                            start=(fi == 0), stop=(fi == HG - 1)),
                            reads=[("wd", db), ("act", fi, t)], writes=[("ps", b)])
            for j in range(2):
                dc = dg * 2 + j
                for t in range(NTT):
                    tsl = slice(t * TT, (t + 1) * TT)
                    b = j * 4 + t
                    S.op("dve", lambda e, dc=dc, tsl=tsl, b=b: e.scalar_tensor_tensor(
                        C.xT[:, dc, tsl], C.ps[b][:], 0.5, C.xT[:, dc, tsl], ALU.mult, ALU.add),
                        reads=[("ps", b), ("x", dc, t)], writes=[("x", dc, t)])


def load_x(C, x_d):
    S = C.S
    for dc in range(NDC):
        S.op("sp", lambda e, dc=dc: e.dma_start(out=C.xT[:, dc, :], in_=x_d[dc * 128:(dc + 1) * 128, :]),
             writes=[("x", dc, t) for t in range(NTT)], dma=True)


def store_x(C, o_d):
    S = C.S
    for dc in range(NDC):
        S.op("sp", lambda e, dc=dc: e.dma_start(out=o_d[dc * 128:(dc + 1) * 128, :], in_=C.xT[:, dc, :]),
             reads=[("x", dc, t) for t in range(NTT)], dma=True)


def load_gains(C, g_d):
    C.S.op("sp", lambda e: e.dma_start(out=C.gains[:], in_=g_d), writes=[("gains",)], dma=True)


def emit_h(C, h_d):
    S = C.S
    rmsnorm_T(C, 1, C.hn, "hn")
    for dc in range(NDC):
        S.op("sp", lambda e, dc=dc: e.dma_start(out=h_d[dc * 128:(dc + 1) * 128, :], in_=C.hn[:, dc, :]),
             reads=[("hn", dc, t) for t in range(NTT)], dma=True)


def wo_apply(C, y_d, wo_d):
    S = C.S
    wo = C.act[:, 0:4, :].rearrange("p a (b c) -> p (a b) c", c=D)
    C.S.op("pool", lambda e: e.dma_start(out=wo, in_=wo_d, max_dma_last_dim=4096),
           writes=[("act", fi, t) for fi in range(4) for t in range(NTT)], dma=True)
    for jc in range(NDC):
        S.op("sp", lambda e, jc=jc: e.dma_start(out=C.hn[:, jc, :], in_=y_d[jc * 128:(jc + 1) * 128, :]),
             writes=[("hn", jc, t) for t in range(NTT)], dma=True)
    wkeys = [("act", fi, t) for fi in range(4) for t in range(NTT)]
    for dc in range(NDC):
        for t in range(NTT):
            tsl = slice(t * TT, (t + 1) * TT)
            b = C.cnt["ps"] % 8
            C.cnt["ps"] += 1
            for jc in range(NDC):
                S.op("pe", lambda e, dc=dc, jc=jc, tsl=tsl, b=b: e.matmul(
                    C.ps[b][:], wo[:, jc, dc * 128:(dc + 1) * 128], C.hn[:, jc, tsl],
                    start=(jc == 0), stop=(jc == NDC - 1)),
                    reads=wkeys + [("hn", jc, t)], writes=[("ps", b)])
            S.op("dve", lambda e, dc=dc, tsl=tsl, b=b: e.tensor_tensor(
                C.xT[:, dc, tsl], C.ps[b][:], C.xT[:, dc, tsl], ALU.add),
                reads=[("ps", b), ("x", dc, t)], writes=[("x", dc, t)])


import math

STOPAT = 0
SEQ = 4096
NCH = SEQ // 128
PI = math.pi
TWO_PI = 2.0 * math.pi
C1 = 6.28125
C2 = TWO_PI - C1
O_GN, O_QLAT, O_KVLAT, O_QNN, O_QNR, O_KNN, O_KNR, NREP = 0, 256, 512, 640, 704, 736, 800, 832


class B:
    pass


def alloc_b(nc, S, ps=None):
    b = B()
    b.nc, b.S = nc, S
    b.ps = ps if ps is not None else [nc.alloc_psum_tensor("bps%d" % i, [128, 512], F32) for i in range(8)]
    A = nc.alloc_sbuf_tensor
    b.win = A("b_win", [128, 8, 1440], BF16)
    b.wuq = A("b_wuq", [128, 2, 384], BF16)
    b.wukv = A("b_wukv", [128, 512], BF16)
    b.rep = A("b_rep", [128, NREP], F32)
    b.rc = A("b_rc", [128, 8], F32)
    b.invf = A("b_invf", [128, 80], F32)
    b.invn = A("b_invn", [128, 16], F32)
    b.epsb = A("b_eps", [128, 1], F32)
    b.posi = A("b_posi", [128, NCH], I32)
    b.posf = A("b_posf", [128, NCH], F32)
    b.cosR = A("b_cosR", [128, NCH, 64], F32)
    b.sinR = A("b_sinR", [128, NCH, 64], F32)
    b.cosM = A("b_cosM", [128, NCH, 16], F32)
    b.sinM = A("b_sinM", [128, NCH, 16], F32)
    b.ident = A("b_ident", [128, 128], BF16)
    b.tri = A("b_tri", [128, 128], BF16)
    b.onesf = A("b_onesf", [128, 128], F32)
    b.hst = [A("b_hst%d" % i, [128, 8, 512], BF16) for i in range(2)]
    b.kfT = A("b_kfT", [128, 4, SEQ], BF16)
    b.tg = [b.hst[i][:].rearrange("p a b -> p (a b)").bitcast(F32).rearrange("p (c f) -> p c f", f=64) for i in range(2)]
    b.tgi = b.kfT[:, 0, :].bitcast(I32).rearrange("p (c f) -> p c f", f=64)
    b.qfT = A("b_qfT", [128, 4, SEQ], BF16)
    b.vall = A("b_vall", [128, NCH, 4, 66], BF16)
    b.Sf = A("b_Sf", [128, 2, 128], F32)
    b.Sb = A("b_Sb", [128, 2, 128], BF16)

    def two(name, shape, dt):
        return [A("b_%s%d" % (name, i), shape, dt) for i in range(2)]
    b.tA = two("tA", [128, 4, 2, 64], F32)
    b.tB = two("tB", [128, 4, 2, 64], F32)
    b.qk = two("qk", [128, 4, 128], BF16)
    b.kz = two("kz", [128, 2, 128], BF16)
    b.vr = two("vr", [128, 256], BF16)
    b.sg = two("sg", [128, 256], F32)
    b.qkT = two("qkT", [128, 4, 128], BF16)
    b.sTm = two("sTm", [128, 2, 128], BF16)
    b.ysb = two("ysb", [128, 2, 128], F32)
    b.s2 = two("s2", [128, 2], F32)
    b.yn = two("yn", [128, 2, 128], F32)
    b.yo = two("yo", [128, 2, 128], BF16)
    b.mixR = two("mixR", [128, 2, 512], BF16)
    b.s3 = two("s3", [128, 4], F32)
    b.cqn = two("cqn", [128, 384], BF16)
    b.krn = two("krn", [128, 1, 2, 16], F32)
    b.krt = two("krt", [128, 2, 1, 2, 16], F32)
    b.latT = two("latT", [128, 3, 128], BF16)
    b.s8 = two("s8", [128, 8], F32)
    b.qtmp = two("qtmp", [128, 4, 64], F32)
    b.qrn = two("qrn", [128, 4, 2, 16], F32)
    b.qrt = two("qrt", [128, 2, 4, 2, 16], F32)
    b.qf = two("qf", [128, 4, 128], BF16)
    b.s4 = two("s4", [128, 4], F32)
    b.ktmp = two("ktmp", [128, 4, 64], F32)
    b.kf = two("kf", [128, 4, 128], BF16)
    b.junk = two("junk", [128, 256], F32)
    b.pT = [A("b_pT%d" % i, [128, 512], BF16) for i in range(4)]
    b.orec = two("orec", [128, 4], F32)
    b.omix = two("omix", [128, 4, 256], BF16)
    b.mixM = two("mixM", [128, 2, 512], BF16)
    return b


def bc(ap, shape, axes):
    for a in axes:
        ap = ap.unsqueeze(a)
    return ap.to_broadcast(shape)


def load_b_consts(b, win_d, wuq_d, wukv_d, rep_d, rc_d, invf_d, invn_d, pos_d):
    S = b.S
    S.op("pool", lambda e: e.dma_start(out=b.win[:], in_=win_d, max_dma_last_dim=4096), writes=["b_win"], dma=True)
    S.op("pool", lambda e: e.dma_start(out=b.wuq[:], in_=wuq_d), writes=["b_wuq"], dma=True)
    S.op("pool", lambda e: e.dma_start(out=b.wukv[:], in_=wukv_d), writes=["b_wukv"], dma=True)
    S.op("sp", lambda e: e.dma_start(out=b.rep[:], in_=rep_d), writes=["b_rep"], dma=True)
    S.op("sp", lambda e: e.dma_start(out=b.rc[:], in_=rc_d), writes=["b_rc"], dma=True)
    S.op("sp", lambda e: e.dma_start(out=b.invf[:], in_=invf_d), writes=["b_invf"], dma=True)
    S.op("sp", lambda e: e.dma_start(out=b.invn[:], in_=invn_d), writes=["b_invn"], dma=True)
    S.op("sp", lambda e: e.dma_start(out=b.posi[:], in_=pos_d), writes=["b_posi"], dma=True)


def init_b_consts(b):
    S = b.S
    S.op("pool", lambda e: e.memset(b.onesf[:], 1.0), writes=["b_onesf"])
    S.op("pool", lambda e: e.memset(b.epsb[:], EPS), writes=["b_eps"])
    S.op("pool", lambda e: e.affine_select(b.ident[:], b.onesf[:], [[1, 128]], ALU.is_equal, 0.0, base=0, channel_multiplier=-1),
         reads=["b_onesf"], writes=["b_ident"])
    S.op("pool", lambda e: e.affine_select(b.tri[:], b.onesf[:], [[1, 128]], ALU.is_ge, 0.0, base=0, channel_multiplier=-1),
         reads=["b_onesf"], writes=["b_tri"])
    S.op("pool", lambda e: e.memset(b.vall[:, :, :, 64:66], 1.0), writes=["b_vones"])
    for i in range(2):
        S.op("pool", lambda e, i=i: e.memset(b.qf[i][:, :, 96:128], 0.0), writes=[("qfpad", i)])
        S.op("pool", lambda e, i=i: e.memset(b.kf[i][:, :, 96:128], 0.0), writes=[("kfpad", i)])


def rope_tables(b):
    S = b.S
    K0 = ["b_tg0"] + [("hst", 0, dc) for dc in range(8)]
    K1 = ["b_tg1"] + [("hst", 1, dc) for dc in range(8)]
    KI = ["b_tgi"] + [("kfT", c) for c in range(NCH)]
    S.op("dve", lambda e: e.tensor_copy(b.posf[:], b.posi[:]), reads=["b_posi"], writes=["b_posf"])
    for (nf, off, cosT, sinT, nm) in [(64, 0, b.cosR, b.sinR, "R"), (16, 64, b.cosM, b.sinM, "M")]:
        shp = [128, NCH, nf]
        ang, tmp = b.tg[0][:, :, 0:nf], b.tg[1][:, :, 0:nf]
        ti = b.tgi[:, :, 0:nf]
        for (shift, dst, dn) in [(0.0, sinT, "sin"), (PI / 2, cosT, "cos")]:
            S.op("dve", lambda e, ang=ang, shp=shp, off=off, nf=nf: e.tensor_tensor(
                ang, bc(b.posf[:, :], shp, [2]), bc(b.invf[:, off:off + nf], shp, [1]), ALU.mult),
                reads=["b_posf", "b_invf"], writes=K0)
            if shift != 0.0:
                S.op("dve", lambda e, ang=ang, shift=shift: e.tensor_scalar_add(ang, ang, shift), writes=K0)
            S.op("dve", lambda e, ang=ang, tmp=tmp: e.tensor_scalar_mul(tmp, ang, 1.0 / TWO_PI), writes=K0 + K1)
            S.op("dve", lambda e, ti=ti, tmp=tmp: e.tensor_copy(ti, tmp), writes=K1 + KI)
            S.op("dve", lambda e, ti=ti, tmp=tmp: e.tensor_copy(tmp, ti), writes=K1 + KI)
            S.op("dve", lambda e, ang=ang, tmp=tmp: e.scalar_tensor_tensor(ang, tmp, -C1, ang, ALU.mult, ALU.add),
                 writes=K0 + K1)
            S.op("dve", lambda e, ang=ang, tmp=tmp: e.scalar_tensor_tensor(ang, tmp, -C2, ang, ALU.mult, ALU.add),
                 writes=K0 + K1)
            S.op("dve", lambda e, ang=ang: e.tensor_scalar(ang, ang, PI, -PI, ALU.min, ALU.max), writes=K0)
            S.op("act", lambda e, ang=ang, dst=dst: e.activation(dst[:], ang, AF.Sin), writes=K0 + ["b_tab" + nm + dn])


def rope_tok(b, eng2, x_all, x1, x2, cos_all, sin_h, tA, tB, o1, o2, rk, wk_tmp, wk_out):
    S = b.S
    S.op("dve", lambda e: e.tensor_tensor(tA[:], x_all, cos_all, ALU.mult), reads=rk, writes=[wk_tmp + "A"])
    S.op("dve", lambda e: e.tensor_tensor(tB[:, :, 0, :], x2, sin_h, ALU.mult), reads=rk, writes=[wk_tmp + "B0"])
    S.op("dve", lambda e: e.tensor_tensor(tB[:, :, 1, :], x1, sin_h, ALU.mult), reads=rk, writes=[wk_tmp + "B1"])
    S.op(eng2, lambda e: e.tensor_tensor(o1, tA[:, :, 0, :], tB[:, :, 0, :], ALU.subtract),
         reads=[wk_tmp + "A", wk_tmp + "B0"], writes=[wk_out + "1"])
    S.op(eng2, lambda e: e.tensor_tensor(o2, tA[:, :, 1, :], tB[:, :, 1, :], ALU.add),
         reads=[wk_tmp + "A", wk_tmp + "B1"], writes=[wk_out + "2"])


def rstd_small(b, ss, n, invn_sl, key):
    S = b.S
    S.op("dve", lambda e: e.tensor_tensor(ss[:, 0:n], ss[:, 0:n], invn_sl, ALU.mult), reads=[key, "b_invn"], writes=[key])
    S.op("act", lambda e: e.activation(ss[:, 0:n], ss[:, 0:n], AF.Sqrt, bias=b.epsb[:], scale=1.0), reads=[key, "b_eps"], writes=[key])
    S.op("dve", lambda e: e.reciprocal(ss[:, 0:n], ss[:, 0:n]), reads=[key], writes=[key])


def b1_chunk(b, c, h_d, y_d):
    S = b.S
    i = c % 2
    ps = b.ps
    hb = (c // 4) % 2
    hst = b.hst[hb]
    if c % 4 == 0:
        half, off = c // 16, (c % 16) * 128
        for dc in range(8):
            S.op("sp", lambda e, dc=dc, hst=hst, half=half, off=off: e.dma_start(
                out=hst[:, dc, :], in_=h_d[half, dc * 128:(dc + 1) * 128, off:off + 512]),
                writes=[("hst", hb, dc)], dma=True)
    tsl = slice((c % 4) * 128, (c % 4 + 1) * 128)
    for (pb, c0, c1) in [(0, 0, 512), (1, 512, 1024), (2, 1024, 1440)]:
        for dc in range(8):
            S.op("pe", lambda e, pb=pb, c0=c0, c1=c1, dc=dc, hst=hst, tsl=tsl: e.matmul(
                ps[pb][:, 0:c1 - c0], hst[:, dc, tsl], b.win[:, dc, c0:c1], start=(dc == 0), stop=(dc == 7)),
                reads=[("hst", hb, dc), "b_win"], writes=[("ps", pb)])
    if STOPAT == 1:
        return
    P1 = ps[0][:].rearrange("p (g t h) -> p g t h", g=4, t=2)
    qk = b.qk[i]
    cosb = bc(b.cosR[:, c, :], [128, 4, 2, 64], [1, 1])
    sinb = bc(b.sinR[:, c, :], [128, 4, 64], [1])
    qkv = qk[:].rearrange("p g (t h) -> p g t h", t=2)
    rope_tok(b, "dve", P1, P1[:, :, 0, :], P1[:, :, 1, :], cosb, sinb, b.tA[i], b.tB[i],
             qkv[:, :, 0, :], qkv[:, :, 1, :], [("ps", 0), "b_tabRcos", "b_tabRsin"], "tR%d" % i, "qk%d_" % i)
    if STOPAT == 2:
        return
    qk_keys = ["qk%d_1" % i, "qk%d_2" % i]
    for h in range(2):
        S.op("dve", lambda e, h=h: e.tensor_scalar(b.kz[i][:, h, :], qk[:, 2 + h, :], b.rc[:, h:h + 1], None, ALU.mult),
             reads=qk_keys + ["b_rc"], writes=[("kz", i, h)])
    S.op("act", lambda e: e.copy(b.vr[i][:], ps[1][:, 0:256]), reads=[("ps", 1)], writes=[("vr", i)])
    S.op("act", lambda e: e.activation(b.sg[i][:], ps[1][:, 256:512], AF.Silu), reads=[("ps", 1)], writes=[("sg", i)])
    p7b = ps[7][:].bitcast(BF16)
    for j in range(4):
        S.op("pe", lambda e, j=j: e.transpose(p7b[:, j * 128:(j + 1) * 128], qk[:, j, :], b.ident[:]),
             reads=qk_keys + ["b_ident"], writes=[("ps", 7)])
    S.op("act", lambda e: e.copy(b.qkT[i][:].rearrange("p a b -> p (a b)"), p7b[:, 0:512]), reads=[("ps", 7)], writes=[("qkT", i)])
    for h in range(2):
        S.op("pe", lambda e, h=h: e.matmul(ps[7][:, 256 + h * 128:256 + (h + 1) * 128], b.qkT[i][:, 2 + h, :], b.qkT[i][:, h, :],
                                           start=True, stop=True),
             reads=[("qkT", i)], writes=[("ps", 7)])
    for h in range(2):
        S.op("dve", lambda e, h=h: e.scalar_tensor_tensor(b.sTm[i][:, h, :], ps[7][:, 256 + h * 128:256 + (h + 1) * 128],
                                                          b.rc[:, 2 + h:3 + h], b.tri[:], ALU.mult, ALU.mult),
             reads=[("ps", 7), "b_rc", "b_tri"], writes=[("sTm", i, h)])
    if STOPAT == 3:
        return
    for h in range(2):
        S.op("pe", lambda e, h=h: e.matmul(ps[0][:, h * 128:(h + 1) * 128], b.sTm[i][:, h, :], b.vr[i][:, h * 128:(h + 1) * 128],
                                           start=True, stop=False),
             reads=[("sTm", i, h), ("vr", i)], writes=[("ps", 0)])
        S.op("pe", lambda e, h=h: e.matmul(ps[0][:, h * 128:(h + 1) * 128], b.qkT[i][:, h, :], b.Sb[:, h, :],
                                           start=False, stop=True),
             reads=[("qkT", i), ("Sb", h)], writes=[("ps", 0)])
    for h in range(2):
        S.op("pe", lambda e, h=h: e.matmul(ps[0][:, 256 + h * 128:256 + (h + 1) * 128], b.kz[i][:, h, :], b.vr[i][:, h * 128:(h + 1) * 128],
                                           start=True, stop=True),
             reads=[("kz", i, h), ("vr", i)], writes=[("ps", 0)])
    for h in range(2):
        S.op("dve", lambda e, h=h: e.tensor_scalar(b.ysb[i][:, h, :], ps[0][:, h * 128:(h + 1) * 128], b.rc[:, 4 + h:5 + h], None, ALU.mult),
             reads=[("ps", 0), "b_rc"], writes=[("ysb", i, h)])
        S.op("act", lambda e, h=h: e.activation(b.junk[i][:, 0:128], b.ysb[i][:, h, :], AF.Square, accum_out=b.s2[i][:, h:h + 1]),
             reads=[("ysb", i, h)], writes=[("junk", i), ("s2", i, h)])
    for h in range(2):
        S.op("dve", lambda e, h=h: e.scalar_tensor_tensor(b.Sf[:, h, :], b.Sf[:, h, :], b.rc[:, 6 + h:7 + h],
                                                          ps[0][:, 256 + h * 128:256 + (h + 1) * 128], ALU.mult, ALU.add),
             reads=[("Sf", h), ("ps", 0), "b_rc"], writes=[("Sf", h)])
        S.op("act", lambda e, h=h: e.copy(b.Sb[:, h, :], b.Sf[:, h, :]), reads=[("Sf", h)], writes=[("Sb", h)])
    if STOPAT == 4:
        return
    s2k = [("s2", i, 0), ("s2", i, 1)]
    S.op("dve", lambda e: e.tensor_scalar_mul(b.s2[i][:], b.s2[i][:], 1.0 / 128), reads=s2k, writes=s2k)
    S.op("act", lambda e: e.activation(b.s2[i][:], b.s2[i][:], AF.Sqrt, bias=b.epsb[:], scale=1.0), reads=s2k + ["b_eps"], writes=s2k)
    S.op("dve", lambda e: e.reciprocal(b.s2[i][:], b.s2[i][:]), reads=s2k, writes=s2k)
    for h in range(2):
        S.op("dve", lambda e, h=h: e.scalar_tensor_tensor(b.yn[i][:, h, :], b.ysb[i][:, h, :], b.s2[i][:, h:h + 1],
                                                          b.rep[:, O_GN + h * 128:O_GN + (h + 1) * 128], ALU.mult, ALU.mult),
             reads=[("ysb", i, h), "b_rep"] + s2k, writes=[("yn", i, h)])
        S.op("dve", lambda e, h=h: e.tensor_tensor(b.yo[i][:, h, :], b.yn[i][:, h, :], b.sg[i][:, h * 128:(h + 1) * 128], ALU.mult),
             reads=[("yn", i, h), ("sg", i)], writes=[("yo", i, h)])
    p4b = ps[4][:].bitcast(BF16)
    mb = (c // 4) % 2
    for h in range(2):
        S.op("pe", lambda e, h=h: e.transpose(p4b[:, 768 + h * 128:768 + (h + 1) * 128], b.yo[i][:, h, :], b.ident[:]),
             reads=[("yo", i, h), "b_ident"], writes=[("ps", 4)])
    S.op("act", lambda e: e.copy(b.mixR[mb][:, :, tsl], p4b[:, 768:1024].rearrange("p (h t) -> p h t", h=2)),
         reads=[("ps", 4)], writes=[("mixR", mb, c % 4)])
    if c % 4 == 3:
        half, off = c // 16, ((c // 4) % 4) * 512
        for h in range(2):
            S.op("sp", lambda e, h=h, half=half, off=off: e.dma_start(
                out=y_d[half, h * 128:(h + 1) * 128, off:off + 512], in_=b.mixR[mb][:, h, :]),
                reads=[("mixR", mb, k) for k in range(4)], dma=True)
    if STOPAT == 5:
        return
    P3 = ps[2]
    s3 = b.s3[i]
    for (j, a0, a1) in [(0, 0, 256), (1, 256, 384), (2, 384, 416)]:
        S.op("act", lambda e, j=j, a0=a0, a1=a1: e.activation(b.junk[i][:, 0:a1 - a0], P3[:, a0:a1], AF.Square, accum_out=s3[:, j:j + 1]),
             reads=[("ps", 2)], writes=[("junk", i), ("s3", i)])
    rstd_small(b, s3, 3, b.invn[:, 0:3], ("s3", i))
    cqn = b.cqn[i]
    S.op("dve", lambda e: e.scalar_tensor_tensor(cqn[:, 0:256], P3[:, 0:256], s3[:, 0:1], b.rep[:, O_QLAT:O_QLAT + 256], ALU.mult, ALU.mult),
         reads=[("ps", 2), ("s3", i), "b_rep"], writes=[("cqn", i, 0)])
    S.op("dve", lambda e: e.scalar_tensor_tensor(cqn[:, 256:384], P3[:, 256:384], s3[:, 1:2], b.rep[:, O_KVLAT:O_KVLAT + 128], ALU.mult, ALU.mult),
         reads=[("ps", 2), ("s3", i), "b_rep"], writes=[("cqn", i, 1)])
    krn = b.krn[i]
    S.op("dve", lambda e: e.scalar_tensor_tensor(krn[:].rearrange("p a t h -> p (a t h)"), P3[:, 384:416], s3[:, 2:3],
                                                 b.rep[:, O_KNR:O_KNR + 32], ALU.mult, ALU.mult),
         reads=[("ps", 2), ("s3", i), "b_rep"], writes=[("krn", i)])
    kf = b.kf[i]
    kro = kf[:, 0:1, 64:96].rearrange("p g (t h) -> p g t h", t=2)
    cosm = bc(b.cosM[:, c, :], [128, 1, 2, 16], [1, 1])
    sinm = bc(b.sinM[:, c, :], [128, 1, 16], [1])
    krt = b.krt[i]
    rope_tok(b, "dve", krn[:], krn[:, :, 0, :], krn[:, :, 1, :], cosm, sinm, krt[:, 0], krt[:, 1],
             kro[:, :, 0, :], kro[:, :, 1, :], [("krn", i), "b_tabMcos", "b_tabMsin"], "tK%d" % i, "kro%d_" % i)
    S.op("dve", lambda e: e.tensor_copy(kf[:, 1:4, 64:96], kf[:, 0:1, 64:96].to_broadcast([128, 3, 32])),
         reads=["kro%d_1" % i, "kro%d_2" % i], writes=[("kfr", i)])
    if STOPAT == 6:
        return
    p3b = ps[3][:].bitcast(BF16)
    for j in range(3):
        S.op("pe", lambda e, j=j: e.transpose(p3b[:, j * 128:(j + 1) * 128], cqn[:, j * 128:(j + 1) * 128], b.ident[:]),
             reads=[("cqn", i, 0 if j < 2 else 1), "b_ident"], writes=[("ps", 3)])
    S.op("act", lambda e: e.copy(b.latT[i][:].rearrange("p a b -> p (a b)"), p3b[:, 0:384]), reads=[("ps", 3)], writes=[("latT", i)])
    for j in range(2):
        S.op("pe", lambda e, j=j: e.matmul(ps[4][:, 0:384], b.latT[i][:, j, :], b.wuq[:, j, :], start=(j == 0), stop=(j == 1)),
             reads=[("latT", i), "b_wuq"], writes=[("ps", 4)])
    S.op("pe", lambda e: e.matmul(ps[5][:], b.latT[i][:, 2, :], b.wukv[:], start=True, stop=True),
         reads=[("latT", i), "b_wukv"], writes=[("ps", 5)])
    if STOPAT == 7:
        return
    qup = ps[4][:, 0:384].rearrange("p (g d) -> p g d", g=4)
    s8 = b.s8[i]
    for h in range(4):
        S.op("act", lambda e, h=h: e.activation(b.junk[i][:, 0:64], qup[:, h, 0:64], AF.Square, accum_out=s8[:, h:h + 1]),
             reads=[("ps", 4)], writes=[("junk", i), ("s8", i)])
        S.op("act", lambda e, h=h: e.activation(b.junk[i][:, 0:32], qup[:, h, 64:96], AF.Square, accum_out=s8[:, 4 + h:5 + h]),
             reads=[("ps", 4)], writes=[("junk", i), ("s8", i)])
    rstd_small(b, s8, 8, b.invn[:, 4:12], ("s8", i))
    qf = b.qf[i]
    S.op("dve", lambda e: e.tensor_tensor(b.qtmp[i][:], qup[:, :, 0:64], bc(s8[:, 0:4], [128, 4, 64], [2]), ALU.mult),
         reads=[("ps", 4), ("s8", i)], writes=[("qtmp", i)])
    S.op("dve", lambda e: e.tensor_tensor(qf[:, :, 0:64], b.qtmp[i][:], bc(b.rep[:, O_QNN:O_QNN + 64], [128, 4, 64], [1]), ALU.mult),
         reads=[("qtmp", i), "b_rep"], writes=[("qfn", i)])
    qrn = b.qrn[i]
    qrn3 = qrn[:].rearrange("p g t h -> p g (t h)")
    S.op("dve", lambda e: e.tensor_tensor(qrn3, qup[:, :, 64:96], bc(s8[:, 4:8], [128, 4, 32], [2]), ALU.mult),
         reads=[("ps", 4), ("s8", i)], writes=[("qrn", i)])
    S.op("dve", lambda e: e.tensor_tensor(qrn3, qrn3, bc(b.rep[:, O_QNR:O_QNR + 32], [128, 4, 32], [1]), ALU.mult),
         reads=[("qrn", i), "b_rep"], writes=[("qrn", i)])
    qro = qf[:, :, 64:96].rearrange("p g (t h) -> p g t h", t=2)
    cosm4 = bc(b.cosM[:, c, :], [128, 4, 2, 16], [1, 1])
    sinm4 = bc(b.sinM[:, c, :], [128, 4, 16], [1])
    qrt = b.qrt[i]
    rope_tok(b, "dve", qrn[:], qrn[:, :, 0, :], qrn[:, :, 1, :], cosm4, sinm4, qrt[:, 0], qrt[:, 1],
             qro[:, :, 0, :], qro[:, :, 1, :], [("qrn", i), "b_tabMcos", "b_tabMsin"], "tQ%d" % i, "qro%d_" % i)
    if STOPAT == 8:
        return
    kvup = ps[5][:].rearrange("p (g d) -> p g d", g=4)
    s4 = b.s4[i]
    for h in range(4):
        S.op("act", lambda e, h=h: e.activation(b.junk[i][:, 0:64], kvup[:, h, 0:64], AF.Square, accum_out=s4[:, h:h + 1]),
             reads=[("ps", 5)], writes=[("junk", i), ("s4", i)])
    rstd_small(b, s4, 4, b.invn[:, 12:16], ("s4", i))
    S.op("dve", lambda e: e.tensor_tensor(b.ktmp[i][:], kvup[:, :, 0:64], bc(s4[:], [128, 4, 64], [2]), ALU.mult),
         reads=[("ps", 5), ("s4", i)], writes=[("ktmp", i)])
    S.op("dve", lambda e: e.tensor_tensor(kf[:, :, 0:64], b.ktmp[i][:], bc(b.rep[:, O_KNN:O_KNN + 64], [128, 4, 64], [1]), ALU.mult),
         reads=[("ktmp", i), "b_rep"], writes=[("kfn", i)])
    S.op("act", lambda e: e.copy(b.vall[:, c, :, 0:64], kvup[:, :, 64:128]), reads=[("ps", 5)], writes=[("vall", c)])
    if STOPAT == 9:
        return
    p6b = ps[6][:].bitcast(BF16)
    qkeys = [("qfn", i), "qro%d_1" % i, "qro%d_2" % i]
    kkeys = [("kfn", i), ("kfr", i), "kro%d_1" % i, "kro%d_2" % i]
    for h in range(4):
        S.op("pe", lambda e, h=h: e.transpose(p6b[:, h * 128:(h + 1) * 128], qf[:, h, :], b.ident[:]),
             reads=qkeys + ["b_ident", ("qfpad", i)], writes=[("ps", 6)])
    if STOPAT == 10:
        csl = slice(c * 128, (c + 1) * 128)
        S.op("dve", lambda e: e.tensor_copy(b.qfT[:, :, csl], p6b[:, 0:512].rearrange("p (h t) -> p h t", h=4)),
             reads=[("ps", 6)], writes=[("qfT", c)])
        return
    if STOPAT == 11:
        return
    for h in range(4):
        S.op("pe", lambda e, h=h: e.transpose(p6b[:, 512 + h * 128:512 + (h + 1) * 128], kf[:, h, :], b.ident[:]),
             reads=kkeys + ["b_ident", ("kfpad", i)], writes=[("ps", 6)])
    csl = slice(c * 128, (c + 1) * 128)
    S.op("dve", lambda e: e.tensor_copy(b.qfT[:, :, csl], p6b[:, 0:512].rearrange("p (h t) -> p h t", h=4)),
         reads=[("ps", 6)], writes=[("qfT", c)])
    S.op("dve", lambda e: e.tensor_copy(b.kfT[:, :, csl], p6b[:, 512:1024].rearrange("p (h t) -> p h t", h=4)),
         reads=[("ps", 6)], writes=[("kfT", c)])


def b2_tile(b, t, y_d):
    S = b.S
    ps = b.ps
    oi = t % 2
    omix = b.omix[oi]
    cnt = getattr(b, "_b2cnt", 0)
    for h in range(4):
        nkb = 4 * t + 4
        for kb in range(nkb):
            j = max(0, kb - 4 * t)
            n0 = j * 128
            w = 512 - n0
            sb = cnt % 2
            pi = cnt % 4
            cnt += 1
            pT = b.pT[pi]
            S.op("pe", lambda e, h=h, kb=kb, n0=n0, w=w, sb=sb: e.matmul(
                ps[sb][:, 0:w], b.kfT[:, h, kb * 128:(kb + 1) * 128], b.qfT[:, h, t * 512 + n0:(t + 1) * 512],
                start=True, stop=True),
                reads=[("kfT", kb)] + [("qfT", 4 * t + q) for q in range(j, 4)], writes=[("ps", sb)])
            S.op("act", lambda e, w=w, sb=sb, pT=pT: e.activation(pT[:, 0:w], ps[sb][:, 0:w], AF.Exp, scale=96.0 ** -0.5),
                 reads=[("ps", sb)], writes=[("pT", pi)])
            if kb >= 4 * t:
                S.op("dve", lambda e, pT=pT: e.tensor_tensor(pT[:, 0:128], pT[:, 0:128], b.tri[:], ALU.mult),
                     reads=[("pT", pi), "b_tri"], writes=[("pT", pi)])
            for qb in range(j, 4):
                S.op("pe", lambda e, h=h, kb=kb, qb=qb, n0=n0, pT=pT: e.matmul(
                    ps[2 + qb][:, 0:66], pT[:, qb * 128 - n0:(qb + 1) * 128 - n0], b.vall[:, kb, h, :],
                    start=(kb == 0), stop=(kb == 4 * t + qb)),
                    reads=[("pT", pi), ("vall", kb), "b_vones"], writes=[("ps", 2 + qb)])
        orec = b.orec[oi]
        for qb in range(4):
            S.op("dve", lambda e, qb=qb: e.reciprocal(orec[:, qb:qb + 1], ps[2 + qb][:, 64:65]),
                 reads=[("ps", 2 + qb)], writes=[("orec", oi, qb)])
            S.op("dve", lambda e, qb=qb, h=h: e.tensor_scalar(omix[:, qb, h * 64:(h + 1) * 64], ps[2 + qb][:, 0:64],
                                                               orec[:, qb:qb + 1], None, ALU.mult),
                 reads=[("ps", 2 + qb), ("orec", oi, qb)], writes=[("omix", oi, qb, h)])
    b._b2cnt = cnt
    p6b = ps[6][:].bitcast(BF16)
    for jc in range(2):
        for qb in range(4):
            S.op("pe", lambda e, jc=jc, qb=qb: e.transpose(p6b[:, (jc * 4 + qb) * 128:(jc * 4 + qb + 1) * 128],
                                                          omix[:, qb, jc * 128:(jc + 1) * 128], b.ident[:]),
                 reads=[("omix", oi, qb, 2 * jc), ("omix", oi, qb, 2 * jc + 1), "b_ident"], writes=[("ps", 6)])
    S.op("act", lambda e: e.copy(b.mixM[oi][:].rearrange("p a b -> p (a b)"), p6b[:, 0:1024]), reads=[("ps", 6)], writes=[("mixM", oi)])
    half, off = t // 4, (t % 4) * 512
    for jc in range(2):
        S.op("sp", lambda e, jc=jc: e.dma_start(out=y_d[half, 256 + jc * 128:256 + (jc + 1) * 128, off:off + 512], in_=b.mixM[oi][:, jc, :]),
             reads=[("mixM", oi)], dma=True)


def init_state(b):
    S = b.S
    S.op("pool", lambda e: e.memset(b.Sf[:], 0.0), writes=[("Sf", 0), ("Sf", 1)])
    S.op("pool", lambda e: e.memset(b.Sb[:], 0.0), writes=[("Sb", 0), ("Sb", 1)])


def mixer(b, h_d, y_d):
    init_state(b)
    for c in range(NCH):
        b1_chunk(b, c, h_d, y_d)
    for t in range(8):
        b2_tile(b, t, y_d)


def prep_b_inputs(inp, l, b, g, h_full_T=None):
    w = inp["w_in"][l]
    cols = []
    for base in (0, 512):
        for hh in (2 * g, 2 * g + 1):
            cols += list(range(base + hh * 128, base + (hh + 1) * 128))
    for base in (1024, 1536):
        for hh in (2 * g, 2 * g + 1):
            cols += list(range(base + hh * 128, base + (hh + 1) * 128))
    cols += list(range(2048, 2464))
    win = np.ascontiguousarray(w[:, cols].reshape(8, 128, 1440).transpose(1, 0, 2))
    wuq = np.ascontiguousarray(inp["w_uq"][l][:, 4 * g * 96:(4 * g + 4) * 96].reshape(2, 128, 384).transpose(1, 0, 2))
    wukv = np.ascontiguousarray(inp["w_ukv"][l][:, 4 * g * 128:(4 * g + 4) * 128])
    rep = np.concatenate([inp["ret_head_norm"][l][2 * g], inp["ret_head_norm"][l][2 * g + 1], inp["q_lat_norm"][l],
                          inp["kv_lat_norm"][l], inp["qn_nope"][l], inp["qn_rope"][l], inp["kn_nope"][l], inp["kn_rope"][l]]).astype(np.float32)
    rep = np.ascontiguousarray(np.broadcast_to(rep[None, :], (128, rep.shape[0])))
    return dict(b_win_d=win, b_wuq_d=wuq, b_wukv_d=wukv, b_rep_d=rep)

def const_b_inputs(g):
    p = np.arange(128, dtype=np.float64)
    rc = np.zeros((128, 8), np.float64)
    for hl in range(2):
        lg = np.log(1.0 - 2.0 ** (-5.0 - (2 * g + hl)))
        sc = 128.0 ** -0.5
        rc[:, 0 + hl] = np.exp((127 - p) * lg) * sc
        rc[:, 2 + hl] = np.exp(-(p + 1) * lg) * sc
        rc[:, 4 + hl] = np.exp((p + 1) * lg)
        rc[:, 6 + hl] = np.exp(128 * lg)
    invR = (1.0 / (np.float32(10000.0) ** (np.arange(0, 128, 2, dtype=np.float32) / np.float32(128)))).astype(np.float32)
    invM = (1.0 / (np.float32(10000.0) ** (np.arange(0, 32, 2, dtype=np.float32) / np.float32(32)))).astype(np.float32)
    invf = np.ascontiguousarray(np.broadcast_to(np.concatenate([invR, invM])[None], (128, 80))).astype(np.float32)
    invn = np.array([1 / 256, 1 / 128, 1 / 32, 0] + [1 / 64] * 4 + [1 / 32] * 4 + [1 / 64] * 4, np.float32)
    invn = np.ascontiguousarray(np.broadcast_to(invn[None], (128, 16)))
    return dict(b_rc_d=rc.astype(np.float32), b_invf_d=invf, b_invn_d=invn)

def pos_layout(pos_b):
    return np.ascontiguousarray(pos_b.reshape(32, 128).T.astype(np.int32))


from concourse.bass_utils import run_bass_kernel_spmd
import ml_dtypes

_PROGS = {}


def _dram_in(nc, n, s, dt):
    return nc.dram_tensor(n, s, dt, kind="ExternalInput").ap()


def build_A():
    nc = bass.Bass("TRN2", target_bir_lowering=False)
    xin = _dram_in(nc, "xin", [1024, 2048], F32)
    wgu = _dram_in(nc, "wgu", [22, 128, 2, 8, 128], F32)
    wd = _dram_in(nc, "wd", [4, 22, 128, 256], F32)
    gains = _dram_in(nc, "gains", [128, 3, 8], F32)
    xo = nc.dram_tensor("xo", [1024, 2048], F32, kind="ExternalOutput").ap()
    ho = nc.dram_tensor("ho", [1024, 2048], BF16, kind="ExternalOutput").ap()
    S = Sched(nc)
    C = alloc_common(nc, S)
    C.epsb = nc.alloc_sbuf_tensor("epsb", [128, 1], F32)
    S.op("pool", lambda e: e.memset(C.epsb[:], EPS), writes=[("epsb",)])
    load_gains(C, gains)
    load_x(C, xin)
    ffn(C, wgu, wd, 0)
    store_x(C, xo)
    emit_h(C, ho)
    S.emit()
    return nc


def build_C():
    nc = bass.Bass("TRN2", target_bir_lowering=False)
    xin = _dram_in(nc, "xin", [1024, 2048], F32)
    yin = _dram_in(nc, "yin", [1024, 2048], BF16)
    wo = _dram_in(nc, "wo", [128, 8, 1024], F32)
    wgu = _dram_in(nc, "wgu", [22, 128, 2, 8, 128], F32)
    wd = _dram_in(nc, "wd", [4, 22, 128, 256], F32)
    gains = _dram_in(nc, "gains", [128, 3, 8], F32)
    xo = nc.dram_tensor("xo", [1024, 2048], F32, kind="ExternalOutput").ap()
    S = Sched(nc)
    C = alloc_common(nc, S)
    C.epsb = nc.alloc_sbuf_tensor("epsb", [128, 1], F32)
    S.op("pool", lambda e: e.memset(C.epsb[:], EPS), writes=[("epsb",)])
    load_gains(C, gains)
    load_x(C, xin)
    wo_apply(C, yin, wo)
    ffn(C, wgu, wd, 2)
    store_x(C, xo)
    S.emit()
    return nc


def build_B():
    nc = bass.Bass("TRN2", target_bir_lowering=False)
    h_d = _dram_in(nc, "b_h_d", [2, 1024, 2048], BF16)
    win_d = _dram_in(nc, "b_win_d", [128, 8, 1440], F32)
    wuq_d = _dram_in(nc, "b_wuq_d", [128, 2, 384], F32)
    wukv_d = _dram_in(nc, "b_wukv_d", [128, 512], F32)
    rep_d = _dram_in(nc, "b_rep_d", [128, NREP], F32)
    rc_d = _dram_in(nc, "b_rc_d", [128, 8], F32)
    invf_d = _dram_in(nc, "b_invf_d", [128, 80], F32)
    invn_d = _dram_in(nc, "b_invn_d", [128, 16], F32)
    pos_d = _dram_in(nc, "b_pos_d", [128, 32], I32)
    y_d = nc.dram_tensor("b_y_d", [2, 512, 2048], BF16, kind="ExternalOutput").ap()
    S = Sched(nc)
    b = alloc_b(nc, S)
    load_b_consts(b, win_d, wuq_d, wukv_d, rep_d, rc_d, invf_d, invn_d, pos_d)
    init_b_consts(b)
    rope_tables(b)
    mixer(b, h_d, y_d)
    S.emit()
    return nc


def prep_ffn_w(wg, wu, wd):
    wgr = wg.reshape(8, 128, 22, 128).transpose(2, 1, 0, 3)
    wur = wu.reshape(8, 128, 22, 128).transpose(2, 1, 0, 3)
    wgu = np.ascontiguousarray(np.stack([wgr, wur], axis=2))
    wdr = np.ascontiguousarray(wd.reshape(22, 128, 4, 256).transpose(2, 0, 1, 3))
    return wgu, wdr


def prep_wo(wo):
    perm = []
    for gp in range(2):
        perm += list(range(gp * 256, gp * 256 + 256))
        perm += list(range(512 + gp * 256, 512 + gp * 256 + 256))
    return np.ascontiguousarray(wo[perm].reshape(8, 128, 1024).transpose(1, 0, 2))


def _get(name, fn):
    if name not in _PROGS:
        _PROGS[name] = fn()
    return _PROGS[name]


def kernel(**inputs):
    inp = {k: np.asarray(v) for k, v in inputs.items()}
    x = inp["x"].astype(np.float32, copy=False)
    NCORE = 8
    cores = list(range(NCORE))
    xT = [np.ascontiguousarray(x[c // 2, (c % 2) * 2048:(c % 2 + 1) * 2048, :].T) for c in cores]
    pA, pB, pC = _get("A", build_A), _get("B", build_B), _get("C", build_C)
    for l in range(2):
        gains = np.stack([inp["ffn1_norm"][l], inp["mix_norm"][l], inp["ffn2_norm"][l]], 0).astype(np.float32)
        gains_r = np.ascontiguousarray(gains.reshape(3, 8, 128).transpose(2, 0, 1))
        wgu1, wd1 = prep_ffn_w(inp["ffn1_w_gate"][l], inp["ffn1_w_up"][l], inp["ffn1_w_down"][l])
        res = run_bass_kernel_spmd(pA, [dict(xin=xT[c], wgu=wgu1, wd=wd1, gains=gains_r) for c in cores], core_ids=cores)
        xT = [res.results[c]["xo"] for c in cores]
        hT = [res.results[c]["ho"] for c in cores]
        del wgu1, wd1
        maps = []
        for c in cores:
            bb, g = c // 2, c % 2
            m = dict(b_h_d=np.ascontiguousarray(np.stack([hT[2 * bb], hT[2 * bb + 1]], 0)),
                     b_pos_d=pos_layout(inp["positions"][bb]))
            m.update(prep_b_inputs(inp, l, bb, g))
            m.update(const_b_inputs(g))
            maps.append(m)
        res = run_bass_kernel_spmd(pB, maps, core_ids=cores)
        yb = [res.results[c]["b_y_d"] for c in cores]
        wgu2, wd2 = prep_ffn_w(inp["ffn2_w_gate"][l], inp["ffn2_w_up"][l], inp["ffn2_w_down"][l])
        wo_r = prep_wo(inp["w_o"][l])
        maps = []
        for c in cores:
            bb, g = c // 2, c % 2
            yin = np.ascontiguousarray(np.concatenate([yb[2 * bb][g], yb[2 * bb + 1][g]], 0))
            maps.append(dict(xin=xT[c], yin=yin, wo=wo_r, wgu=wgu2, wd=wd2, gains=gains_r))
        res = run_bass_kernel_spmd(pC, maps, core_ids=cores)
        xT = [res.results[c]["xo"] for c in cores]
        del wgu2, wd2
    out = np.empty((4, 4096, 1024), np.float32)
    for c in cores:
        out[c // 2, (c % 2) * 2048:(c % 2 + 1) * 2048, :] = np.asarray(xT[c], dtype=np.float32).T
    return out
```
